# Optimizing a Trainium2 kernel written in Bass

```python
import math
import jax
import jax.numpy as jnp
from jax import lax
import numpy as np

D_MODEL = 1024
BATCH = 1
SEQ = 16384
DEPTH = 4

GRID_W = 64
CTX_LEN = 256
HEAD_DIM = 64
SCALE = HEAD_DIM ** -0.5
ROPE_AXIS_DIM = HEAD_DIM // 2
ROPE_THETA = 10000.0
Q_BLOCK = 128
DA_HEADS = 4
DA_QK = DA_HEADS * 2 * HEAD_DIM
DA_V = DA_HEADS * 2 * HEAD_DIM
GQA_HEADS = 4
GQA_KV_HEADS = 2
GQA_GROUP = GQA_HEADS // GQA_KV_HEADS
GQA_Q = GQA_HEADS * HEAD_DIM
GQA_KV = GQA_KV_HEADS * HEAD_DIM
LRU_WIDTH = 256
LRU_BLOCKS = 4
LRU_BLOCK = LRU_WIDTH // LRU_BLOCKS
CONV_W = 4
LRU_C = 8.0
MIX_WIDTH = DA_V + GQA_Q + LRU_WIDTH
IN_SPLITS = (DA_QK, DA_QK, DA_V, GQA_Q, GQA_KV, GQA_KV, LRU_WIDTH, LRU_WIDTH)
IN_WIDTH = 2560
D_FF = 2816
N_MOD = 9
EPS = 1e-6

kernel_name = 'hymba_style_diffattn_gqa_rglru_macaron_dit'


def rms_norm(x, g):
    xf = x.astype(jnp.float32)
    y = xf * lax.rsqrt(jnp.mean(xf * xf, axis=-1, keepdims=True) + EPS)
    return y.astype(x.dtype) * g


def modulate(x, shift, scale):
    return x * (1.0 + scale) + shift


def swiglu(x, w13, w2):
    a, b = jnp.split(x @ w13, 2, axis=-1)
    return (jax.nn.silu(a) * b) @ w2


def axial_rope_tables(n):
    rows = n // GRID_W
    row = jnp.repeat(jnp.arange(rows, dtype=jnp.float32), GRID_W)
    col = jnp.tile(jnp.arange(GRID_W, dtype=jnp.float32), rows)
    half = ROPE_AXIS_DIM // 2
    inv = ROPE_THETA ** (-jnp.arange(half, dtype=jnp.float32) / half)
    ang = jnp.stack([row[:, None] * inv, col[:, None] * inv], axis=0)
    return jnp.cos(ang), jnp.sin(ang)


def apply_axial_rope(x, cos, sin):
    shape = (2, 1, x.shape[1]) + (1,) * (x.ndim - 3) + (cos.shape[-1],)
    cos = cos.reshape(shape).astype(x.dtype)
    sin = sin.reshape(shape).astype(x.dtype)
    xa = jnp.stack(jnp.split(x, 2, axis=-1), axis=0)
    x1, x2 = jnp.split(xa, 2, axis=-1)
    out = jnp.concatenate([x1 * cos - x2 * sin, x1 * sin + x2 * cos], axis=-1)
    return jnp.concatenate([out[0], out[1]], axis=-1)


def over_query_blocks(fn, qs):
    b, n = qs[0].shape[:2]
    nb = n // Q_BLOCK
    blocks = tuple(jnp.moveaxis(q.reshape((b, nb, Q_BLOCK) + q.shape[2:]), 1, 0) for q in qs)
    out = lax.map(lambda qb: fn(*qb), blocks)
    return jnp.moveaxis(out, 0, 1).reshape((b, n) + out.shape[3:])


def diff_attn_core(q1, q2, k1, k2, v, lam, subln_g, lam_init):
    s1 = jnp.einsum('bqhd,bkhd->bhqk', q1, k1, preferred_element_type=jnp.float32) * SCALE
    s2 = jnp.einsum('bqhd,bkhd->bhqk', q2, k2, preferred_element_type=jnp.float32) * SCALE
    p = jax.nn.softmax(s1, axis=-1) - lam * jax.nn.softmax(s2, axis=-1)
    o = jnp.einsum('bhqk,bkhe->bqhe', p.astype(v.dtype), v)
    return rms_norm(o, subln_g) * (1.0 - lam_init)


def gqa_core(q, k, v):
    s = jnp.einsum('bqngd,bknd->bngqk', q, k, preferred_element_type=jnp.float32) * SCALE
    p = jax.nn.softmax(s, axis=-1)
    return jnp.einsum('bngqk,bknd->bqngd', p.astype(v.dtype), v)


def depthwise_conv(x, w, b):
    y = lax.conv_general_dilated(x, w[:, None, :], window_strides=(1,), padding=[(1, 2)],
                                 dimension_numbers=('NWC', 'WIO', 'NWC'),
                                 feature_group_count=x.shape[-1])
    return y + b


def block_diag(x, w):
    b, n, _ = x.shape
    y = jnp.einsum('bnkc,kcd->bnkd', x.reshape(b, n, LRU_BLOCKS, LRU_BLOCK), w)
    return y.reshape(b, n, LRU_WIDTH)


def rglru_coeffs(x, wa, ba, wi, bi, lam):
    r = jax.nn.sigmoid(block_diag(x, wa).astype(jnp.float32) + ba.astype(jnp.float32))
    i = jax.nn.sigmoid(block_diag(x, wi).astype(jnp.float32) + bi.astype(jnp.float32))
    log_a = -LRU_C * r * jax.nn.softplus(-lam.astype(jnp.float32))
    a = jnp.exp(log_a)
    u = jnp.sqrt(-jnp.expm1(2.0 * log_a)) * (i * x.astype(jnp.float32))
    return a, u


def _lin_combine(left, right):
    a1, b1 = left
    a2, b2 = right
    return a1 * a2, a2 * b1 + b2


def linear_scan(a, u, h0, reverse):
    if reverse:
        a = jnp.flip(a, axis=1)
        u = jnp.flip(u, axis=1)
    u = u.at[:, 0].add(a[:, 0] * h0)
    _, h = lax.associative_scan(_lin_combine, (a, u), axis=1)
    return jnp.flip(h, axis=1) if reverse else h


def hybrid_mixer(z_lat, z_ctx, w_in, w_out, da_lam, da_subln_g, lam_init, qk_norm_g,
                 conv_w, conv_b, wa, ba, wi, bi, lru_lambda, cos, sin, ctx_out):
    b, n, _ = z_lat.shape
    m = z_ctx.shape[1]
    cuts = [int(v) for v in np.cumsum(IN_SPLITS)[:-1]]
    lat = jnp.split(z_lat @ w_in, cuts, axis=-1)
    cx = jnp.split(z_ctx @ w_in, cuts, axis=-1)

    def da_split(qa, ka, va, length):
        q = qa.reshape(b, length, DA_HEADS, 2, HEAD_DIM)
        k = ka.reshape(b, length, DA_HEADS, 2, HEAD_DIM)
        v = va.reshape(b, length, DA_HEADS, 2 * HEAD_DIM)
        return q[..., 0, :], q[..., 1, :], k[..., 0, :], k[..., 1, :], v

    lq1, lq2, lk1, lk2, lv = da_split(lat[0], lat[1], lat[2], n)
    cq1, cq2, ck1, ck2, cv = da_split(cx[0], cx[1], cx[2], m)
    lq1, lq2, lk1, lk2 = [apply_axial_rope(t, cos, sin) for t in (lq1, lq2, lk1, lk2)]
    lf = da_lam.astype(jnp.float32)
    lam = jnp.exp(jnp.sum(lf[0] * lf[1])) - jnp.exp(jnp.sum(lf[2] * lf[3])) + lam_init
    k1_all = jnp.concatenate([ck1, lk1], axis=1)
    k2_all = jnp.concatenate([ck2, lk2], axis=1)
    v_all = jnp.concatenate([cv, lv], axis=1)
    a_lat = over_query_blocks(
        lambda q1, q2: diff_attn_core(q1, q2, k1_all, k2_all, v_all, lam, da_subln_g, lam_init),
        (lq1, lq2))

    def gqa_split(qa, ka, va, length):
        q = rms_norm(qa.reshape(b, length, GQA_HEADS, HEAD_DIM), qk_norm_g[0])
        k = rms_norm(ka.reshape(b, length, GQA_KV_HEADS, HEAD_DIM), qk_norm_g[1])
        return q, k, va.reshape(b, length, GQA_KV_HEADS, HEAD_DIM)

    gq, gk, gv = gqa_split(lat[3], lat[4], lat[5], n)
    cgq, cgk, cgv = gqa_split(cx[3], cx[4], cx[5], m)
    gq = apply_axial_rope(gq, cos, sin).reshape(b, n, GQA_KV_HEADS, GQA_GROUP, HEAD_DIM)
    gk = apply_axial_rope(gk, cos, sin)
    gk_all = jnp.concatenate([cgk, gk], axis=1)
    gv_all = jnp.concatenate([cgv, gv], axis=1)
    g_lat = over_query_blocks(lambda q: gqa_core(q, gk_all, gv_all), (gq,))

    xl = depthwise_conv(lat[6], conv_w, conv_b)
    xc = depthwise_conv(cx[6], conv_w, conv_b)
    h_lat_dirs = []
    h_ctx_dirs = []
    for d, rev in ((0, False), (1, True)):
        a_c, u_c = rglru_coeffs(xc, wa[d], ba[d], wi[d], bi[d], lru_lambda[d])
        h_c = linear_scan(a_c, u_c, jnp.zeros((b, LRU_WIDTH), jnp.float32), rev)
        a_l, u_l = rglru_coeffs(xl, wa[d], ba[d], wi[d], bi[d], lru_lambda[d])
        h_l = linear_scan(a_l, u_l, h_c[:, 0] if rev else h_c[:, -1], rev)
        h_lat_dirs.append(h_l)
        h_ctx_dirs.append(h_c)
    r_lat = (h_lat_dirs[0] + h_lat_dirs[1]).astype(z_lat.dtype) * jax.nn.gelu(lat[7])

    y_lat = jnp.concatenate([a_lat.reshape(b, n, DA_V), g_lat.reshape(b, n, GQA_Q), r_lat],
                            axis=-1) @ w_out
    if not ctx_out:
        return y_lat, None
    a_ctx = diff_attn_core(cq1, cq2, ck1, ck2, cv, lam, da_subln_g, lam_init)
    g_ctx = gqa_core(cgq.reshape(b, m, GQA_KV_HEADS, GQA_GROUP, HEAD_DIM), cgk, cgv)
    r_ctx = (h_ctx_dirs[0] + h_ctx_dirs[1]).astype(z_ctx.dtype) * jax.nn.gelu(cx[7])
    y_ctx = jnp.concatenate([a_ctx.reshape(b, m, DA_V), g_ctx.reshape(b, m, GQA_Q), r_ctx],
                            axis=-1) @ w_out
    return y_lat, y_ctx


def setup_inputs(seed: int = 0) -> dict:
    key = jax.random.key(seed)
    ks = iter(jax.random.split(key, 32))
    f32 = jnp.float32
    L, D = DEPTH, D_MODEL

    def nrm(shape, scale):
        return jax.random.normal(next(ks), shape, f32) * scale

    u = jax.random.uniform(next(ks), (L, 2, LRU_WIDTH), f32, 0.9, 0.999)
    sig = u ** (1.0 / LRU_C)
    lru_lambda = jnp.log(sig) - jnp.log1p(-sig)
    return {
        'x': nrm((BATCH, SEQ, D), 1.0),
        'c': nrm((BATCH, D), 1.0),
        'ctx': nrm((BATCH, CTX_LEN, D), 1.0),
        'c_ctx': nrm((D,), 1.0),
        'ada_w': nrm((L, D, N_MOD * D), 0.5 * D ** -0.5),
        'ada_b': nrm((L, N_MOD * D), 0.02),
        'norm_g': 1.0 + nrm((L, 3, D), 0.01),
        'ffn1_w13': nrm((L, D, 2 * D_FF), D ** -0.5),
        'ffn1_w2': nrm((L, D_FF, D), D_FF ** -0.5),
        'ffn2_w13': nrm((L, D, 2 * D_FF), D ** -0.5),
        'ffn2_w2': nrm((L, D_FF, D), D_FF ** -0.5),
        'w_in': nrm((L, D, IN_WIDTH), D ** -0.5),
        'w_out': nrm((L, MIX_WIDTH, D), MIX_WIDTH ** -0.5),
        'da_lam': nrm((L, 4, HEAD_DIM), 0.1),
        'da_subln_g': 1.0 + nrm((L, 2 * HEAD_DIM), 0.01),
        'qk_norm_g': 1.0 + nrm((L, 2, HEAD_DIM), 0.01),
        'lru_conv_w': nrm((L, CONV_W, LRU_WIDTH), CONV_W ** -0.5),
        'lru_conv_b': nrm((L, LRU_WIDTH), 0.02),
        'lru_wa': nrm((L, 2, LRU_BLOCKS, LRU_BLOCK, LRU_BLOCK), LRU_BLOCK ** -0.5),
        'lru_ba': nrm((L, 2, LRU_WIDTH), 0.02),
        'lru_wi': nrm((L, 2, LRU_BLOCKS, LRU_BLOCK, LRU_BLOCK), LRU_BLOCK ** -0.5),
        'lru_bi': nrm((L, 2, LRU_WIDTH), 0.02),
        'lru_lambda': lru_lambda,
        'final_g': 1.0 + nrm((D,), 0.01),
    }


def reference(x, c, ctx, c_ctx, ada_w, ada_b, norm_g, ffn1_w13, ffn1_w2, ffn2_w13, ffn2_w2,
              w_in, w_out, da_lam, da_subln_g, qk_norm_g, lru_conv_w, lru_conv_b,
              lru_wa, lru_ba, lru_wi, lru_bi, lru_lambda, final_g):
    b, n, d_model = x.shape
    cos, sin = axial_rope_tables(n)

    def ffn_step(s, mods, w13, w2, g, i0):
        y = swiglu(modulate(rms_norm(s, g), mods[i0], mods[i0 + 1]), w13, w2)
        return s + 0.5 * mods[i0 + 2] * y

    h, hc = x, ctx
    for l in range(DEPTH):
        last = l == DEPTH - 1
        lam_init = 0.8 - 0.6 * math.exp(-0.3 * l)
        m_lat = (jax.nn.silu(c) @ ada_w[l] + ada_b[l]).reshape(b, N_MOD, d_model)
        m_ctx = (jax.nn.silu(c_ctx) @ ada_w[l] + ada_b[l]).reshape(N_MOD, d_model)
        ml = [m_lat[:, k, None, :] for k in range(N_MOD)]
        mc = [m_ctx[k] for k in range(N_MOD)]

        h = ffn_step(h, ml, ffn1_w13[l], ffn1_w2[l], norm_g[l, 0], 0)
        hc = ffn_step(hc, mc, ffn1_w13[l], ffn1_w2[l], norm_g[l, 0], 0)

        zl = modulate(rms_norm(h, norm_g[l, 1]), ml[3], ml[4])
        zc = modulate(rms_norm(hc, norm_g[l, 1]), mc[3], mc[4])
        yl, yc = hybrid_mixer(zl, zc, w_in[l], w_out[l], da_lam[l], da_subln_g[l], lam_init,
                              qk_norm_g[l], lru_conv_w[l], lru_conv_b[l], lru_wa[l], lru_ba[l],
                              lru_wi[l], lru_bi[l], lru_lambda[l], cos, sin, not last)
        h = h + ml[5] * yl

        h = ffn_step(h, ml, ffn2_w13[l], ffn2_w2[l], norm_g[l, 2], 6)
        if not last:
            hc = hc + mc[5] * yc
            hc = ffn_step(hc, mc, ffn2_w13[l], ffn2_w2[l], norm_g[l, 2], 6)
    return rms_norm(h, final_g)
```

```python
import math
from contextlib import ExitStack
import numpy as np
import concourse.bass as bass
import concourse.mybir as mybir
from concourse.bass_utils import run_bass_kernel_spmd

F32 = mybir.dt.float32
BF16 = mybir.dt.bfloat16
AF = mybir.ActivationFunctionType
ALU = mybir.AluOpType
AX = mybir.AxisListType

NCORES = 8
D = 1024
SEQ = 16384
TL = SEQ // NCORES
CT = 256
NT = TL + CT
DEPTH = 4
DFF = 2816
NFC = DFF // 128
EPS = 1e-6
GRID_W = 64
NBLK_IN = 26


class _Ctr:
    def __init__(self, sem, step):
        self.sem = sem
        self.val = 0
        self.step = step


class Sy:
    def __init__(self, nc, es):
        self.nc = nc
        self.eng = {}
        for name, h in (("pe", nc.tensor), ("act", nc.scalar), ("dve", nc.vector), ("pool", nc.gpsimd), ("sp", nc.sync)):
            c = _Ctr(es.enter_context(nc.semaphore("sem_" + name)), 1)
            c.h = h
            c.waited = {}
            c.name = name
            self.eng[name] = c
        self.dsem = {}
        for q in ("sp", "pool"):
            self.dsem[q] = [_Ctr(es.enter_context(nc.semaphore("dma_%s_%d" % (q, i))), 16) for i in range(12)]
        self.dnext = {"sp": 0, "pool": 0}
        self.cc = _Ctr(es.enter_context(nc.semaphore("sem_cc")), 1)
        self.st = {}

    def _wait(self, e, ctr, val):
        if val <= 0:
            return
        if ctr is e and getattr(e, "name", "") == "pe":
            return
        if e.waited.get(ctr, 0) >= val:
            return
        e.h.wait_ge(ctr.sem, val)
        e.waited[ctr] = val

    def _deps(self, e, reads, writes):
        for k in reads:
            s = self.st.get(k)
            if s is not None and s[0] is not None:
                self._wait(e, s[0][0], s[0][1])
            if s is not None and isinstance(k, tuple) and k[0] == "ps":
                for c, v in s[1].items():
                    if c is not e:
                        self._wait(e, c, v)
        for k in writes:
            s = self.st.get(k)
            if s is not None:
                if s[0] is not None:
                    self._wait(e, s[0][0], s[0][1])
                for c, v in s[1].items():
                    self._wait(e, c, v)

    def _mark(self, ctr, val, reads, writes):
        for k in reads:
            s = self.st.setdefault(k, [None, {}])
            s[1][ctr] = val
        for k in writes:
            self.st[k] = [(ctr, val), {}]

    def op(self, en, fn, reads=(), writes=()):
        e = self.eng[en]
        self._deps(e, reads, writes)
        ins = fn()
        e.val += 1
        ins.then_inc(e.sem, 1)
        self._mark(e, e.val, reads, writes)

    def dma(self, q, out, in_, reads=(), writes=()):
        e = self.eng[q]
        self._deps(e, reads, writes)
        i = self.dnext[q]
        self.dnext[q] = (i + 1) % len(self.dsem[q])
        c = self.dsem[q][i]
        self._wait(e, c, c.val)
        c.val += 16
        e.h.dma_start(out=out, in_=in_).then_inc(c.sem, 16)
        self._mark(c, c.val, reads, writes)

    def allgather(self, src, dst, reads, writes):
        if getattr(self, "noag", False):
            return
        e = self.eng["pool"]
        self._deps(e, reads, writes)
        c = self.cc
        self._wait(e, c, c.val)
        c.val += 1
        e.h.collective_compute("AllGather", ALU.bypass, replica_groups=[list(range(NCORES))],
                               ins=[src], outs=[dst]).then_inc(c.sem)
        self._mark(c, c.val, reads, writes)

    def barrier(self):
        ctrs = list(self.eng.values()) + self.dsem["sp"] + self.dsem["pool"] + [self.cc]
        for e in self.eng.values():
            for c in ctrs:
                self._wait(e, c, c.val)
        self.st = {}


def lam_init_of(l):
    return 0.8 - 0.6 * math.exp(-0.3 * l)


def build_program(depth=DEPTH, stages=None):
    ST = stages if stages is not None else {'mods', 'ffn1', 'p2', 'p3', 'p4', 'ffn2'}
    nc = bass.Bass("TRN2", target_bir_lowering=False)
    L = DEPTH
    LW = depth

    def din(name, shape, dt=F32):
        return nc.dram_tensor(name, list(shape), dt, kind="ExternalInput")

    xT_d = din("xT", [128, 8, TL])
    cxT_d = din("cxT", [128, 8, CT])
    scT_d = din("scT", [128, 8, 2])
    adaw_d = din("ada_w", [LW, 72, 128, 8, 128])
    adab_d = din("ada_b", [128, L, 72])
    normg_d = din("norm_g", [128, L, 3, 8])
    fing_d = din("final_g", [128, 8])
    w13_d = [din("w13_1", [LW, NFC, 128, 8, 256]), din("w13_2", [LW, NFC, 128, 8, 256])]
    w2_d = [din("w2_1", [LW, 8, 128, NFC, 128]), din("w2_2", [LW, 8, 128, NFC, 128])]
    win_d = din("w_in", [LW, NBLK_IN, 128, 8, 128])
    wv_d = din("w_v", [LW, 128, 8, 640])
    wout_d = din("w_out", [LW, 8, 128, 1024])
    cos_d = din("cosT", [128, TL])
    sin_d = din("sinT", [128, TL])
    dalam_d = din("da_lam", [1, L * 256])
    subg_d = din("subln_g", [128, L])
    qkg_d = din("qkg", [128, L, 4])
    convw_d = din("conv_w", [128, L, 2, 4])
    convb_d = din("conv_b", [128, L, 2])
    wabd_d = din("wa_bd", [LW, 2, 2, 128, 128])
    wibd_d = din("wi_bd", [LW, 2, 2, 128, 128])
    lrub_d = din("lru_b", [128, L, 3, 2, 2])
    mask_d = din("masks", [128, 6, 8])
    out_d = nc.dram_tensor("outT", [128, 8, TL], F32, kind="ExternalOutput")

    XR = 1281
    xch_send = nc.dram_tensor("xch_send", [XR, TL], BF16)
    xch_all = nc.dram_tensor("xch_all", [NCORES * XR, TL], BF16)
    kc_dram = nc.dram_tensor("kc_dram", [5 * 128, CT], BF16)
    vc_dram = nc.dram_tensor("vc_dram", [5 * CT, 128], BF16)
    q_dram = nc.dram_tensor("q_dram", [6 * 128, NT], BF16)
    SMR = 1
    sm_send = nc.dram_tensor("sm_send", [SMR, 1024], F32)
    sm_all = nc.dram_tensor("sm_all", [NCORES * SMR, 1024], F32)

    with ExitStack() as es:
        E = es.enter_context
        S = Sy(nc, es)
        S.noag = ('noag' in ST)

        uid = [0]

        def sb(name, shape, dt, stack=es):
            uid[0] += 1
            return stack.enter_context(nc.sbuf_tensor("%s_%d" % (name, uid[0]), list(shape), dt))

        hT = sb("hT", [128, 8, TL], F32)
        hcT = sb("hcT", [128, 8, CT], F32)
        ones = sb("ones", [128, 128], BF16)
        bd64 = sb("bd64", [128, 128], BF16)
        consts = sb("consts", [128, 3, 2, 3, 8], F32)
        mods = sb("mods", [128, 72, 2], F32)
        sc = sb("sc", [128, 8, 2], F32)
        adab = sb("adab", [128, L, 72], F32)
        normg = sb("normg", [128, L, 3, 8], F32)
        fing = sb("fing", [128, 8], F32)
        lamneg = sb("lamneg", [128, L], F32)
        gsub = sb("gsub", [128, L], F32)
        qkg = sb("qkg_s", [128, L, 4], F32)
        convw = sb("convw", [128, L, 2, 4], F32)
        convb = sb("convb", [128, L, 2], F32)
        lrub = sb("lrub", [128, L, 3, 2, 2], F32)
        nb_r = sb("nb_r", [128, L, 2, 2], F32)
        nb_i = sb("nb_i", [128, L, 2, 2], F32)
        cneg = sb("cneg", [128, L, 2, 2], F32)
        cneg2 = sb("cneg2", [128, L, 2, 2], F32)
        masks = sb("masks_s", [128, 6, 8], F32)
        psum = E(nc.psum_tensor("psum", [128, 8, 512], F32))

        def PS(b):
            return ("ps", b)

        def hap(stream, c, t0, n):
            return (hT[:, c, t0:t0 + n] if stream == 0 else hcT[:, c, t0:t0 + n])

        def hkey(stream, c):
            return ("h", stream, c)

        for c in range(8):
            S.dma("sp", hT[:, c, :], xT_d[:, c, :], writes=[hkey(0, c)])
        S.dma("sp", hcT[:], cxT_d[:, :, :], writes=[hkey(1, c) for c in range(8)])
        small_loads = [(sc, scT_d, "sc"), (adab, adab_d, "adab"), (normg, normg_d, "normg"), (fing, fing_d, "fing"),
                       (gsub, subg_d, "gsub"), (qkg, qkg_d, "qkg"), (convw, convw_d, "convw"), (convb, convb_d, "convb"),
                       (lrub, lrub_d, "lrub"), (masks, mask_d, "masks")]
        for t, d_, k in small_loads:
            S.dma("sp", t[:], d_.ap(), writes=[k])
        S.op("dve", lambda: nc.vector.memset(ones[:], 1.0), writes=["ones"])
        S.op("dve", lambda: nc.vector.memset(bd64[:], 0.0), writes=["bd64"])
        S.op("dve", lambda: nc.vector.memset(bd64[0:64, 0:64], 1.0), writes=["bd64"])
        S.op("dve", lambda: nc.vector.memset(bd64[64:128, 64:128], 1.0), writes=["bd64"])
        S.op("act", lambda: nc.scalar.activation(out=sc[:], in_=sc[:], func=AF.Silu), reads=["sc"], writes=["sc"])
        with ExitStack() as ps_:
            dl = sb("dl", [128, L, 4, 64], F32, ps_)
            dp = sb("dp", [128, L, 2, 64], F32, ps_)
            dsum = sb("dsum", [128, L, 2], F32, ps_)
            src = bass.AP(dalam_d, 0, [[0, 128], [1, L * 256]])
            S.dma("sp", dl[:].rearrange("p l a d -> p (l a d)"), src, writes=["dl"])
            S.op("dve", lambda: nc.vector.tensor_tensor(out=dp[:, :, 0, :], in0=dl[:, :, 0, :], in1=dl[:, :, 1, :], op=ALU.mult),
                 reads=["dl"], writes=["dp"])
            S.op("dve", lambda: nc.vector.tensor_tensor(out=dp[:, :, 1, :], in0=dl[:, :, 2, :], in1=dl[:, :, 3, :], op=ALU.mult),
                 reads=["dl"], writes=["dp"])
            S.op("dve", lambda: nc.vector.tensor_reduce(out=dsum[:], in_=dp[:], axis=AX.X, op=ALU.add), reads=["dp"], writes=["dsum"])
            S.op("act", lambda: nc.scalar.activation(out=dsum[:], in_=dsum[:], func=AF.Exp), reads=["dsum"], writes=["dsum"])
            S.op("dve", lambda: nc.vector.tensor_tensor(out=lamneg[:], in0=dsum[:, :, 1], in1=dsum[:, :, 0], op=ALU.subtract),
                 reads=["dsum"], writes=["lamneg"])
            for l in range(L):
                li = lam_init_of(l)
                S.op("dve", lambda l=l, li=li: nc.vector.tensor_scalar(out=lamneg[:, l:l + 1], in0=lamneg[:, l:l + 1], scalar1=-li,
                                                                        scalar2=None, op0=ALU.add), reads=["lamneg"], writes=["lamneg"])
                S.op("dve", lambda l=l, li=li: nc.vector.tensor_scalar(out=gsub[:, l:l + 1], in0=gsub[:, l:l + 1], scalar1=1.0 - li,
                                                                        scalar2=None, op0=ALU.mult), reads=["gsub"], writes=["gsub"])
            S.op("dve", lambda: nc.vector.tensor_scalar(out=nb_r[:], in0=lrub[:, :, 0, :, :], scalar1=-1.0, scalar2=None, op0=ALU.mult),
                 reads=["lrub"], writes=["nb_r"])
            S.op("dve", lambda: nc.vector.tensor_scalar(out=nb_i[:], in0=lrub[:, :, 1, :, :], scalar1=-1.0, scalar2=None, op0=ALU.mult),
                 reads=["lrub"], writes=["nb_i"])
            S.op("act", lambda: nc.scalar.activation(out=cneg[:], in_=lrub[:, :, 2, :, :], func=AF.Exp, scale=-1.0),
                 reads=["lrub"], writes=["cneg"])
            S.op("act", lambda: nc.scalar.activation(out=cneg[:], in_=cneg[:], func=AF.Ln, bias=1.0, scale=1.0),
                 reads=["cneg"], writes=["cneg"])
            S.op("dve", lambda: nc.vector.tensor_scalar(out=cneg2[:], in0=cneg[:], scalar1=-16.0, scalar2=None, op0=ALU.mult),
                 reads=["cneg"], writes=["cneg2"])
            S.op("dve", lambda: nc.vector.tensor_scalar(out=cneg[:], in0=cneg[:], scalar1=-8.0, scalar2=None, op0=ALU.mult),
                 reads=["cneg"], writes=["cneg"])
            S.barrier()

        def emit_mods(l):
            with ExitStack() as st:
                awb = sb("awb", [128, 3, 8, 128], F32, st)
                mp = psum[:, 7, 0:144]
                for fb in range(72):
                    slot = fb % 3
                    S.dma("sp", awb[:, slot], adaw_d[l, fb], writes=[("awb", slot)])
                    for kc in range(8):
                        S.op("pe", lambda fb=fb, kc=kc, slot=slot: nc.tensor.matmul(
                            psum[:, 7, fb * 2:fb * 2 + 2], lhsT=awb[:, slot, kc, :], rhs=sc[:, kc, :],
                            start=(kc == 0), stop=(kc == 7)), reads=[("awb", slot), "sc"], writes=[PS(7)])
                for s_ in range(2):
                    S.op("dve", lambda s_=s_: nc.vector.tensor_tensor(
                        out=mods[:, :, s_], in0=psum[:, 7, 0:144].rearrange("p (f s) -> p f s", s=2)[:, :, s_],
                        in1=adab[:, l, :], op=ALU.add), reads=[PS(7), "adab"], writes=["mods"])
                for sub in range(3):
                    for s_ in range(2):
                        k0 = sub * 3
                        S.op("dve", lambda sub=sub, s_=s_, k0=k0: nc.vector.scalar_tensor_tensor(
                            out=consts[:, sub, s_, 0, :], in0=mods[:, (k0 + 1) * 8:(k0 + 2) * 8, s_], scalar=1.0,
                            in1=normg[:, l, sub, :], op0=ALU.add, op1=ALU.mult), reads=["mods", "normg"], writes=["consts"])
                        S.op("dve", lambda sub=sub, s_=s_, k0=k0: nc.vector.tensor_copy(
                            out=consts[:, sub, s_, 1, :], in_=mods[:, k0 * 8:(k0 + 1) * 8, s_]), reads=["mods"], writes=["consts"])
                        gm = 1.0 if sub == 1 else 0.5
                        S.op("dve", lambda sub=sub, s_=s_, k0=k0, gm=gm: nc.vector.tensor_scalar(
                            out=consts[:, sub, s_, 2, :], in0=mods[:, (k0 + 2) * 8:(k0 + 3) * 8, s_], scalar1=gm, scalar2=None,
                            op0=ALU.mult), reads=["mods"], writes=["consts"])
                S.barrier()

        def emit_rstd(stream, t0, n, tmp, rstd_out, pbank):
            for c in range(8):
                S.op("act", lambda c=c: nc.scalar.activation(out=tmp["sq"][:, c % 2, 0:n], in_=hap(stream, c, t0, n), func=AF.Square),
                     reads=[hkey(stream, c)], writes=[("sq", c % 2)])
                S.op("pe", lambda c=c: nc.tensor.matmul(psum[:, pbank, 0:n], lhsT=ones[:], rhs=tmp["sq"][:, c % 2, 0:n],
                                                        start=(c == 0), stop=(c == 7)),
                     reads=[("sq", c % 2), "ones"], writes=[PS(pbank)])
            S.op("act", lambda: nc.scalar.activation(out=rstd_out, in_=psum[:, pbank, 0:n], func=AF.Ln, bias=EPS, scale=1.0 / D),
                 reads=[PS(pbank)], writes=["rstd"])
            S.op("act", lambda: nc.scalar.activation(out=rstd_out, in_=rstd_out, func=AF.Exp, scale=-0.5),
                 reads=["rstd"], writes=["rstd"])

        def emit_z(stream, t0, n, sub, zT, zoff, tmp, pbank):
            rstd = tmp["rstd"][:, 0:n]
            emit_rstd(stream, t0, n, tmp, rstd, pbank)
            for c in range(8):
                S.op("dve", lambda c=c: nc.vector.scalar_tensor_tensor(
                    out=tmp["zt"][:, c % 2, 0:n], in0=hap(stream, c, t0, n), scalar=consts[:, sub, stream, 0, c:c + 1],
                    in1=rstd, op0=ALU.mult, op1=ALU.mult), reads=[hkey(stream, c), "rstd", "consts"], writes=[("zt", c % 2)])
                S.op("act", lambda c=c: nc.scalar.activation(
                    out=zT[:, c, zoff:zoff + n], in_=tmp["zt"][:, c % 2, 0:n], func=AF.Identity,
                    bias=consts[:, sub, stream, 1, c:c + 1], scale=1.0), reads=[("zt", c % 2), "consts"], writes=[("zT", c)])

        def make_tmp(st):
            return {"sq": sb("t_sq", [128, 2, 512], BF16, st), "rstd": sb("t_rstd", [128, 512], F32, st),
                    "zt": sb("t_zt", [128, 2, 512], F32, st)}

        def emit_ffn(l, which, do_ctx):
            sub = 0 if which == 0 else 2
            w13 = w13_d[which]
            w2 = w2_d[which]
            groups = [[(0, 0, 512), (0, 512, 512), (0, 1024, 512), (0, 1536, 512)]]
            if do_ctx:
                groups[0].append((1, 0, CT))
            HP = NFC // 2
            with ExitStack() as st:
                tmp = make_tmp(st)
                zT = sb("f_zT", [128, 8, NT], BF16, st)
                gT = sb("f_gT", [128, HP, NT], BF16, st)
                w13b = sb("f_w13", [128, 3, 8, 256], BF16, st)
                w2b = sb("f_w2", [128, 2, HP, 128], BF16, st)
                sa = sb("f_sa", [128, 2, 512], BF16, st)
                n13 = 0
                n2 = 0
                nab = 0
                ny = 0
                for grp in groups:
                    offs = []
                    o = 0
                    for (stream, t0, n) in grp:
                        offs.append(o)
                        emit_z(stream, t0, n, sub, zT, o, tmp, 0)
                        o += n
                    for half in range(2):
                        for ci in range(HP):
                            cc = half * HP + ci
                            slot = n13 % 3
                            n13 += 1
                            S.dma("pool", w13b[:, slot], w13[l, cc], writes=[("w13", slot)])
                            for gi, (stream, t0, n) in enumerate(grp):
                                pa = 1 + 2 * (nab % 2)
                                pb = pa + 1
                                nab += 1
                                for kc in range(8):
                                    S.op("pe", lambda kc=kc, pa=pa, slot=slot, gi=gi, n=n: nc.tensor.matmul(
                                        psum[:, pa, 0:n], lhsT=w13b[:, slot, kc, 0:128], rhs=zT[:, kc, offs[gi]:offs[gi] + n],
                                        start=(kc == 0), stop=(kc == 7)), reads=[("w13", slot), ("zT", kc)], writes=[PS(pa)])
                                for kc in range(8):
                                    S.op("pe", lambda kc=kc, pb=pb, slot=slot, gi=gi, n=n: nc.tensor.matmul(
                                        psum[:, pb, 0:n], lhsT=w13b[:, slot, kc, 128:256], rhs=zT[:, kc, offs[gi]:offs[gi] + n],
                                        start=(kc == 0), stop=(kc == 7)), reads=[("w13", slot), ("zT", kc)], writes=[PS(pb)])
                                ss_ = nab % 2
                                S.op("act", lambda pa=pa, ss_=ss_, n=n: nc.scalar.activation(
                                    out=sa[:, ss_, 0:n], in_=psum[:, pa, 0:n], func=AF.Silu), reads=[PS(pa)], writes=[("sa", ss_)])
                                S.op("dve", lambda pb=pb, ss_=ss_, n=n, ci=ci, gi=gi: nc.vector.tensor_tensor(
                                    out=gT[:, ci, offs[gi]:offs[gi] + n], in0=psum[:, pb, 0:n], in1=sa[:, ss_, 0:n], op=ALU.mult),
                                    reads=[PS(pb), ("sa", ss_)], writes=[("gT", ci)])
                        for j in range(8):
                            slot = n2 % 2
                            n2 += 1
                            S.dma("pool", w2b[:, slot], w2[l, j, :, half * HP:(half + 1) * HP, :], writes=[("w2", slot)])
                            for gi, (stream, t0, n) in enumerate(grp):
                                py = 5 + (ny % 2)
                                ny += 1
                                for ci in range(HP):
                                    S.op("pe", lambda ci=ci, py=py, slot=slot, gi=gi, n=n: nc.tensor.matmul(
                                        psum[:, py, 0:n], lhsT=w2b[:, slot, ci, :], rhs=gT[:, ci, offs[gi]:offs[gi] + n],
                                        start=(ci == 0), stop=(ci == HP - 1)), reads=[("w2", slot), ("gT", ci)], writes=[PS(py)])
                                S.op("dve", lambda py=py, stream=stream, t0=t0, n=n, j=j: nc.vector.scalar_tensor_tensor(
                                    out=hap(stream, j, t0, n), in0=psum[:, py, 0:n], scalar=consts[:, sub, stream, 2, j:j + 1],
                                    in1=hap(stream, j, t0, n), op0=ALU.mult, op1=ALU.add),
                                    reads=[PS(py), hkey(stream, j), "consts"], writes=[hkey(stream, j)])
                S.barrier()

        def emit_mixer(l, last):
            with ExitStack() as mst:
                lx = sb("m_lx", [128, 2, TL + 3], F32, mst)
                lxc = sb("m_lxc", [128, 2, CT + 3], F32, mst)
                lg = sb("m_lg", [128, 2, NT], BF16, mst)
                with ExitStack() as st:
                  if 'p2' in ST:
                    tmp = make_tmp(st)
                    zT = sb("p_zT", [128, 8, 512], BF16, st)
                    wb = sb("p_wb", [128, 4, 8, 128], BF16, st)
                    wv = sb("p_wv", [128, 8, 640], BF16, st)
                    cs = sb("p_cs", [128, 2, 512], F32, st)
                    kst = sb("p_kst", [128, 5, 512], BF16, st)
                    qst = sb("p_qst", [128, 6, 512], BF16, st)
                    vst = sb("p_vst", [128, 2, 640], BF16, st)
                    t1 = sb("p_t1", [128, 512], F32, st)
                    t2 = sb("p_t2", [128, 512], F32, st)
                    t3 = sb("p_t3", [128, 512], F32, st)
                    t4 = sb("p_t4", [128, 512], F32, st)
                    for kc in range(8):
                        S.dma("pool", wv[:, kc, :], wv_d[l, :, kc, :], writes=["wv"])
                    S.op("dve", lambda: nc.vector.memset(lxc[:], 0.0), writes=["lxc"])
                    nw = [0]
                    nv = [0]

                    def st_dma(*a, **k):
                        if 'nostore' in ST:
                            return
                        S.dma(*a, **k)

                    def proj(blk, n, pbank):
                        slot = nw[0] % 4
                        nw[0] += 1
                        S.dma("pool", wb[:, slot], win_d[l, blk], writes=[("wb", slot)])
                        for kc in range(8):
                            S.op("pe", lambda kc=kc: nc.tensor.matmul(psum[:, pbank, 0:n], lhsT=wb[:, slot, kc, :], rhs=zT[:, kc, 0:n],
                                                                     start=(kc == 0), stop=(kc == 7)),
                                 reads=[("wb", slot), ("zT", kc)], writes=[PS(pbank)])

                    subgroups = [(0, i * 512, 512) for i in range(4)] + [(1, 0, CT)]
                    for (stream, t0, n) in subgroups:
                        emit_z(stream, t0, n, 1, zT, 0, tmp, 0)
                        lat = (stream == 0)
                        if lat:
                            S.dma("sp", cs[:, 0, :], cos_d[:, t0:t0 + 512], writes=["cs"])
                            S.dma("sp", cs[:, 1, :], sin_d[:, t0:t0 + 512], writes=["cs"])
                        for kind in range(2):
                            for hh in range(4):
                                dst = (qst[:, hh, 0:n] if kind == 0 else kst[:, hh, 0:n])
                                dkey = ("qst", hh) if kind == 0 else ("kst", hh)
                                proj(kind * 8 + hh, n, 1)
                                if lat:
                                    proj(kind * 8 + 4 + hh, n, 2)
                                    S.op("dve", lambda: nc.vector.tensor_tensor(out=t1[:, 0:n], in0=psum[:, 1, 0:n], in1=cs[:, 0, 0:n], op=ALU.mult),
                                         reads=[PS(1), "cs"], writes=["t1"])
                                    S.op("dve", lambda: nc.vector.tensor_tensor(out=t2[:, 0:n], in0=psum[:, 2, 0:n], in1=cs[:, 1, 0:n], op=ALU.mult),
                                         reads=[PS(2), "cs"], writes=["t2"])
                                    S.op("dve", lambda dst=dst: nc.vector.tensor_tensor(out=dst, in0=t1[:, 0:n], in1=t2[:, 0:n], op=ALU.add),
                                         reads=["t1", "t2"], writes=[dkey])
                                else:
                                    S.op("act", lambda dst=dst: nc.scalar.activation(out=dst, in_=psum[:, 1, 0:n], func=AF.Identity),
                                         reads=[PS(1)], writes=[dkey])
                        if 'half' in ST:
                            continue
                        for (blk, rblk, gi, dst, dkey) in (() if 'x_gqa' in ST else ((16, 18, 0, qst[:, 4, 0:n], ("qst", 4)), (17, 19, 0, qst[:, 5, 0:n], ("qst", 5)),
                                                            (20, 21, 2, kst[:, 4, 0:n], ("kst", 4)))):
                            proj(blk, n, 1)
                            S.op("act", lambda: nc.scalar.activation(out=t4[:, 0:n], in_=psum[:, 1, 0:n], func=AF.Identity),
                                 reads=[PS(1)], writes=["t4"])
                            S.op("act", lambda: nc.scalar.activation(out=tmp["sq"][:, 0, 0:n], in_=t4[:, 0:n], func=AF.Square),
                                 reads=["t4"], writes=[("sq", 0)])
                            S.op("pe", lambda: nc.tensor.matmul(psum[:, 0, 0:n], lhsT=bd64[:], rhs=tmp["sq"][:, 0, 0:n], start=True, stop=True),
                                 reads=[("sq", 0), "bd64"], writes=[PS(0)])
                            S.op("act", lambda: nc.scalar.activation(out=t3[:, 0:n], in_=psum[:, 0, 0:n], func=AF.Ln, bias=EPS, scale=1.0 / 64),
                                 reads=[PS(0)], writes=["t3"])
                            S.op("act", lambda: nc.scalar.activation(out=t3[:, 0:n], in_=t3[:, 0:n], func=AF.Exp, scale=-0.5),
                                 reads=["t3"], writes=["t3"])
                            if lat:
                                proj(rblk, n, 2)
                                S.op("dve", lambda gi=gi: nc.vector.scalar_tensor_tensor(
                                    out=t1[:, 0:n], in0=psum[:, 1, 0:n], scalar=qkg[:, l, gi:gi + 1], in1=cs[:, 0, 0:n],
                                    op0=ALU.mult, op1=ALU.mult), reads=[PS(1), "cs", "qkg"], writes=["t1"])
                                S.op("dve", lambda gi=gi: nc.vector.scalar_tensor_tensor(
                                    out=t2[:, 0:n], in0=psum[:, 2, 0:n], scalar=qkg[:, l, gi + 1:gi + 2], in1=cs[:, 1, 0:n],
                                    op0=ALU.mult, op1=ALU.mult), reads=[PS(2), "cs", "qkg"], writes=["t2"])
                                S.op("dve", lambda: nc.vector.tensor_tensor(out=t1[:, 0:n], in0=t1[:, 0:n], in1=t2[:, 0:n], op=ALU.add),
                                     reads=["t1", "t2"], writes=["t1"])
                                S.op("dve", lambda dst=dst: nc.vector.tensor_tensor(out=dst, in0=t1[:, 0:n], in1=t3[:, 0:n], op=ALU.mult),
                                     reads=["t1", "t3"], writes=[dkey])
                            else:
                                S.op("dve", lambda gi=gi, dst=dst: nc.vector.scalar_tensor_tensor(
                                    out=dst, in0=psum[:, 1, 0:n], scalar=qkg[:, l, gi:gi + 1], in1=t3[:, 0:n],
                                    op0=ALU.mult, op1=ALU.mult), reads=[PS(1), "t3", "qkg"], writes=[dkey])
                        for cb in (() if 'x_lru' in ST else range(2)):
                            proj(22 + cb, n, 1)
                            xdst = (lx[:, cb, 1 + t0:1 + t0 + n] if lat else lxc[:, cb, 1:1 + n])
                            S.op("act", lambda xdst=xdst: nc.scalar.activation(out=xdst, in_=psum[:, 1, 0:n], func=AF.Identity),
                                 reads=[PS(1)], writes=["lx" if lat else "lxc"])
                            proj(24 + cb, n, 2)
                            g0 = (t0 if lat else TL)
                            S.op("act", lambda: nc.scalar.activation(out=t1[:, 0:n], in_=psum[:, 2, 0:n], func=AF.Identity),
                                 reads=[PS(2)], writes=["t1"])
                            S.op("dve", lambda: nc.vector.tensor_tensor(out=t2[:, 0:n], in0=t1[:, 0:n], in1=t1[:, 0:n], op=ALU.mult),
                                 reads=["t1"], writes=["t2"])
                            S.op("dve", lambda: nc.vector.tensor_scalar(out=t2[:, 0:n], in0=t2[:, 0:n], scalar1=0.044715, scalar2=1.0,
                                                                        op0=ALU.mult, op1=ALU.add), reads=["t2"], writes=["t2"])
                            S.op("dve", lambda: nc.vector.tensor_tensor(out=t2[:, 0:n], in0=t2[:, 0:n], in1=t1[:, 0:n], op=ALU.mult),
                                 reads=["t1", "t2"], writes=["t2"])
                            S.op("act", lambda: nc.scalar.activation(out=t2[:, 0:n], in_=t2[:, 0:n], func=AF.Exp, scale=-1.5957691216057308),
                                 reads=["t2"], writes=["t2"])
                            S.op("dve", lambda: nc.vector.tensor_scalar(out=t2[:, 0:n], in0=t2[:, 0:n], scalar1=1.0, scalar2=None, op0=ALU.add),
                                 reads=["t2"], writes=["t2"])
                            S.op("dve", lambda: nc.vector.reciprocal(out=t4[:, 0:n], in_=t2[:, 0:n]), reads=["t2"], writes=["t4"])
                            S.op("dve", lambda cb=cb, g0=g0: nc.vector.tensor_tensor(out=lg[:, cb, g0:g0 + n], in0=t1[:, 0:n], in1=t4[:, 0:n], op=ALU.mult),
                                 reads=["t1", "t4"], writes=["lg"])
                        for tt in (() if 'x_v' in ST else range(n // 128)):
                            vs = nv[0] % 2
                            nv[0] += 1
                            for kc in range(8):
                                S.op("pe", lambda kc=kc, tt=tt: nc.tensor.matmul(psum[:, 4, 0:512], lhsT=zT[:, kc, tt * 128:(tt + 1) * 128],
                                                                               rhs=wv[:, kc, 0:512], start=(kc == 0), stop=(kc == 7)),
                                     reads=[("zT", kc), "wv"], writes=[PS(4)])
                            for kc in range(8):
                                S.op("pe", lambda kc=kc, tt=tt: nc.tensor.matmul(psum[:, 5, 0:128], lhsT=zT[:, kc, tt * 128:(tt + 1) * 128],
                                                                               rhs=wv[:, kc, 512:640], start=(kc == 0), stop=(kc == 7)),
                                     reads=[("zT", kc), "wv"], writes=[PS(5)])
                            S.op("act", lambda vs=vs: nc.scalar.activation(out=vst[:, vs, 0:512], in_=psum[:, 4, 0:512], func=AF.Identity),
                                 reads=[PS(4)], writes=[("vst", vs)])
                            S.op("dve", lambda vs=vs: nc.vector.tensor_copy(out=vst[:, vs, 512:640], in_=psum[:, 5, 0:128]),
                                 reads=[PS(5)], writes=[("vst", vs)])
                            if lat:
                                dstv = xch_send.ap()[640:1280, :].rearrange("q (x d) -> (q x) d", d=128).rearrange(
                                    "(b t) d -> t b d", b=5)[t0 + tt * 128:t0 + (tt + 1) * 128]
                                st_dma("sp", dstv, vst[:, vs, :].rearrange("p (b d) -> p b d", b=5), reads=[("vst", vs)], writes=["xch_send"])
                            else:
                                dstv = vc_dram.ap().rearrange("(b t) d -> t b d", b=5)[tt * 128:(tt + 1) * 128]
                                st_dma("sp", dstv, vst[:, vs, :].rearrange("p (b d) -> p b d", b=5), reads=[("vst", vs)], writes=["vc_dram"])
                        if lat:
                            st_dma("sp", xch_send.ap()[0:640, :].rearrange("(b p) t -> p b t", b=5)[:, :, t0:t0 + n], kst[:, :, 0:n],
                                  reads=[("kst", i) for i in range(5)], writes=["xch_send"])
                            st_dma("sp", q_dram.ap().rearrange("(b p) t -> p b t", b=6)[:, :, t0:t0 + n], qst[:, :, 0:n],
                                  reads=[("qst", i) for i in range(6)], writes=["q_dram"])
                        else:
                            st_dma("sp", kc_dram.ap().rearrange("(b p) t -> p b t", b=5), kst[:, :, 0:n],
                                  reads=[("kst", i) for i in range(5)], writes=["kc_dram"])
                            st_dma("sp", q_dram.ap().rearrange("(b p) t -> p b t", b=6)[:, :, TL:TL + n], qst[:, :, 0:n],
                                  reads=[("qst", i) for i in range(6)], writes=["q_dram"])
                    if 'half' in ST or 'x_lru' in ST:
                        S.op('dve', lambda: nc.vector.memset(lx[:], 0.0), writes=['lx'])
                    hs = sb("p_hs", [128, 8], F32, st)
                    S.op("dve", lambda: nc.vector.memset(hs[:], 0.0), writes=["hs"])
                    for cb in range(2):
                        S.op("dve", lambda cb=cb: nc.vector.tensor_copy(out=hs[:, cb * 3:cb * 3 + 2], in_=lx[:, cb, 1:3]), reads=["lx"], writes=["hs"])
                        S.op("dve", lambda cb=cb: nc.vector.tensor_copy(out=hs[:, cb * 3 + 2:cb * 3 + 3], in_=lx[:, cb, TL:TL + 1]), reads=["lx"], writes=["hs"])
                    hsb = sb("p_hsb", [128, 16], BF16, st)
                    hsr = sb("p_hsr", [128, 8], F32, st)
                    S.op("dve", lambda: nc.vector.tensor_copy(out=hsb[:, 0:8], in_=hs[:]), reads=["hs"], writes=["hsb"])
                    S.op("dve", lambda: nc.vector.tensor_tensor(out=hsr[:], in0=hs[:], in1=hsb[:, 0:8], op=ALU.subtract),
                         reads=["hs", "hsb"], writes=["hsr"])
                    S.op("dve", lambda: nc.vector.tensor_copy(out=hsb[:, 8:16], in_=hsr[:]), reads=["hsr"], writes=["hsb"])
                    st_dma("sp", xch_send.ap()[1280:1281, :].rearrange("o (p c) -> p (o c)", p=128),
                           hsb[:], reads=["hsb"], writes=["xch_send"])
                    S.allgather(xch_send.ap(), xch_all.ap(), reads=["xch_send"], writes=["xch_all"])
                    S.barrier()

                with ExitStack() as st:
                  if 'p3' in ST:
                    xl = sb("r_xl", [128, 2, NT], F32, st)
                    au = sb("r_au", [128, 2, NT], F32, st)
                    hsum = sb("r_hsum", [128, NT], F32, st)
                    hscr = sb("r_hscr", [128, NT], F32, st)
                    rT = sb("r_rT", [128, 2, NT], BF16, st)
                    wab = sb("r_wab", [128, 2, 2, 2, 128], F32, st)
                    hal = sb("r_hal", [128, NCORES, 8], F32, st)
                    ht = sb("r_ht", [128, 8], F32, st)
                    sm = sb("r_sm", [128, 2, 2, 2], F32, st)
                    small = sb("r_small", [128, NCORES, 8], F32, st)
                    aeff = sb("r_aeff", [128, 2, NCORES], F32, st)
                    heff = sb("r_heff", [128, 2, NCORES], F32, st)
                    cend = sb("r_cend", [128, 2, 2], F32, st)
                    sin_ = sb("r_sin", [128, 2, 2], F32, st)
                    rs = sb("r_rs", [128, 8], F32, st)
                    e1 = sb("r_e1", [128, 512], F32, st)
                    e2 = sb("r_e2", [128, 512], F32, st)
                    e3 = sb("r_e3", [128, 512], F32, st)
                    wob = sb("r_wob", [128, 2, 1024], BF16, st)
                    for ai, wd in enumerate((wabd_d, wibd_d)):
                        for d_ in range(2):
                            for cb in range(2):
                                S.dma("sp", wab[:, ai, d_, cb, :], wd[l, d_, cb], writes=["wab"])
                    S.dma("pool", wob[:], wout_d[l, 6:8].rearrange("b p n -> p b n"), writes=["wob"])
                    halb = sb("r_halb", [128, NCORES, 16], BF16, st)
                    for r in range(NCORES):
                        S.dma("sp", halb[:, r, :], xch_all.ap()[r * XR + 1280:r * XR + 1281, :].rearrange(
                            "o (p c) -> p (o c)", p=128), reads=["xch_all"], writes=["halb"])
                    S.op("dve", lambda: nc.vector.tensor_tensor(out=hal[:], in0=halb[:, :, 0:8], in1=halb[:, :, 8:16], op=ALU.add),
                         reads=["halb"], writes=["hal"])
                    for cb in range(2):
                        for (col, mi, src) in ((0, 0, 2), (TL + 1, 1, 0), (TL + 2, 1, 1)):
                            S.op("dve", lambda cb=cb, mi=mi, src=src: nc.vector.tensor_tensor(
                                out=ht[:], in0=hal[:, :, cb * 3 + src], in1=masks[:, mi, :], op=ALU.mult), reads=["hal", "masks"], writes=["ht"])
                            S.op("dve", lambda cb=cb, col=col: nc.vector.tensor_reduce(
                                out=lx[:, cb, col:col + 1], in_=ht[:], axis=AX.X, op=ALU.add), reads=["ht"], writes=["lx"])
                    for cb in range(2):
                        for (srcb, n, o0) in ((lx, TL, 0), (lxc, CT, TL)):
                            key = "lx" if srcb is lx else "lxc"
                            S.op("dve", lambda cb=cb, srcb=srcb, n=n, o0=o0: nc.vector.tensor_scalar(
                                out=xl[:, cb, o0:o0 + n], in0=srcb[:, cb, 0:n], scalar1=convw[:, l, cb, 0:1], scalar2=convb[:, l, cb:cb + 1],
                                op0=ALU.mult, op1=ALU.add), reads=[key, "convw", "convb"], writes=["xl"])
                            for j in range(1, 4):
                                S.op("dve", lambda cb=cb, srcb=srcb, n=n, o0=o0, j=j: nc.vector.scalar_tensor_tensor(
                                    out=xl[:, cb, o0:o0 + n], in0=srcb[:, cb, j:j + n], scalar=convw[:, l, cb, j:j + 1],
                                    in1=xl[:, cb, o0:o0 + n], op0=ALU.mult, op1=ALU.add), reads=[key, "convw", "xl"], writes=["xl"])

                    sgs = [(i * 512, 512) for i in range(4)] + [(TL, CT)]

                    def coeffs(d_, cb):
                        for si, (o0, n) in enumerate(sgs):
                            for ai in range(2):
                                S.op("pe", lambda ai=ai, o0=o0, n=n: nc.tensor.matmul(
                                    psum[:, 1 + ai, 0:n], lhsT=wab[:, ai, d_, cb, :], rhs=xl[:, cb, o0:o0 + n], start=True, stop=True),
                                    reads=["wab", "xl"], writes=[PS(1 + ai)])
                            bar = lrub[:, l, 0, d_, cb:cb + 1]
                            bai = lrub[:, l, 1, d_, cb:cb + 1]
                            S.op("act", lambda n=n, bar=bar: nc.scalar.activation(out=e1[:, 0:n], in_=psum[:, 1, 0:n], func=AF.Sigmoid, scale=1.0, bias=bar),
                                 reads=[PS(1), "lrub"], writes=["e1"])
                            S.op("act", lambda n=n, bai=bai: nc.scalar.activation(out=e2[:, 0:n], in_=psum[:, 2, 0:n], func=AF.Sigmoid, scale=1.0, bias=bai),
                                 reads=[PS(2), "lrub"], writes=["e2"])
                            if si < 4:
                                S.op("dve", lambda n=n, si=si: nc.vector.tensor_reduce(out=rs[:, si:si + 1], in_=e1[:, 0:n], axis=AX.X, op=ALU.add),
                                     reads=["e1"], writes=["rs"])
                            S.op("act", lambda n=n, o0=o0: nc.scalar.activation(out=au[:, 0, o0:o0 + n], in_=e1[:, 0:n], func=AF.Exp,
                                                                                scale=cneg[:, l, d_, cb:cb + 1]), reads=["e1", "cneg"], writes=["a"])
                            S.op("act", lambda n=n: nc.scalar.activation(out=e3[:, 0:n], in_=e1[:, 0:n], func=AF.Exp,
                                                                         scale=cneg2[:, l, d_, cb:cb + 1]), reads=["e1", "cneg2"], writes=["e3"])
                            S.op("act", lambda n=n: nc.scalar.activation(out=e3[:, 0:n], in_=e3[:, 0:n], func=AF.Ln, scale=-1.0, bias=1.0),
                                 reads=["e3"], writes=["e3"])
                            S.op("act", lambda n=n: nc.scalar.activation(out=e3[:, 0:n], in_=e3[:, 0:n], func=AF.Exp, scale=0.5),
                                 reads=["e3"], writes=["e3"])
                            S.op("dve", lambda n=n: nc.vector.tensor_tensor(out=e3[:, 0:n], in0=e3[:, 0:n], in1=e2[:, 0:n], op=ALU.mult),
                                 reads=["e3", "e2"], writes=["e3"])
                            S.op("dve", lambda n=n, o0=o0: nc.vector.tensor_tensor(out=au[:, 1, o0:o0 + n], in0=e3[:, 0:n], in1=xl[:, cb, o0:o0 + n], op=ALU.mult),
                                 reads=["e3", "xl"], writes=["u"])

                    def scan(out_ap, a_ap, u_ap, init, rev, reads, writes):
                        if rev:
                            out_ap, a_ap, u_ap = out_ap[:, ::-1], a_ap[:, ::-1], u_ap[:, ::-1]
                        S.op("dve", lambda: nc.vector.tensor_tensor_scan(out=out_ap, data0=a_ap, data1=u_ap, initial=init,
                                                                         op0=ALU.mult, op1=ALU.add), reads=reads, writes=writes)

                    for d_ in range(2):
                        for cb in range(2):
                            coeffs(d_, cb)
                            rev = (d_ == 1)
                            scan(hscr[:, TL:NT], au[:, 0, TL:NT], au[:, 1, TL:NT], 0.0, rev, ["a", "u"], ["hscr"])
                            ce = (TL if rev else NT - 1)
                            S.op("dve", lambda d_=d_, cb=cb, ce=ce: nc.vector.tensor_copy(out=cend[:, d_, cb:cb + 1], in_=hscr[:, ce:ce + 1]),
                                 reads=["hscr"], writes=["cend"])
                            scan(hscr[:, 0:TL], au[:, 0, 0:TL], au[:, 1, 0:TL], 0.0, rev, ["a", "u"], ["hscr"])
                            le = (0 if rev else TL - 1)
                            S.op("dve", lambda d_=d_, cb=cb, le=le: nc.vector.tensor_copy(out=sm[:, d_, cb, 1:2], in_=hscr[:, le:le + 1]),
                                 reads=["hscr"], writes=["sm"])
                            S.op("dve", lambda d_=d_, cb=cb: nc.vector.tensor_reduce(out=sm[:, d_, cb, 0:1], in_=rs[:, 0:4], axis=AX.X, op=ALU.add),
                                 reads=["rs"], writes=["sm"])
                            S.op("act", lambda d_=d_, cb=cb: nc.scalar.activation(out=sm[:, d_, cb, 0:1], in_=sm[:, d_, cb, 0:1], func=AF.Exp,
                                                                                   scale=cneg[:, l, d_, cb:cb + 1]), reads=["sm", "cneg"], writes=["sm"])
                    S.dma("sp", sm_send.ap()[0:1, 0:1024].rearrange("o (p c) -> p (o c)", p=128), sm[:].rearrange("p a b c -> p (a b c)"),
                          reads=["sm"], writes=["sm_send"])
                    S.allgather(sm_send.ap(), sm_all.ap(), reads=["sm_send"], writes=["sm_all"])
                    for r in range(NCORES):
                        S.dma("sp", small[:, r, :], sm_all.ap()[r * SMR:r * SMR + 1, 0:1024].rearrange("o (p c) -> p (o c)", p=128),
                              reads=["sm_all"], writes=["small"])
                    for d_ in range(2):
                        mi = 2 if d_ == 0 else 4
                        for cb in range(2):
                            ia = (d_ * 2 + cb) * 2
                            S.op("dve", lambda ia=ia, mi=mi: nc.vector.tensor_tensor(out=aeff[:, 0, :], in0=small[:, :, ia], in1=masks[:, mi, :], op=ALU.mult),
                                 reads=["small", "masks"], writes=["aeff"])
                            S.op("dve", lambda mi=mi: nc.vector.tensor_tensor(out=aeff[:, 0, :], in0=aeff[:, 0, :], in1=masks[:, mi + 1, :], op=ALU.add),
                                 reads=["aeff", "masks"], writes=["aeff"])
                            S.op("dve", lambda ia=ia, mi=mi: nc.vector.tensor_tensor(out=heff[:, 0, :], in0=small[:, :, ia + 1], in1=masks[:, mi, :], op=ALU.mult),
                                 reads=["small", "masks"], writes=["heff"])
                            S.op("dve", lambda d_=d_, cb=cb: nc.vector.tensor_copy(out=sin_[:, d_, cb:cb + 1], in_=cend[:, d_, cb:cb + 1]),
                                 reads=["cend"], writes=["sin"])
                            order = range(NCORES) if d_ == 0 else range(NCORES - 1, -1, -1)
                            for r in order:
                                S.op("dve", lambda d_=d_, cb=cb, r=r: nc.vector.scalar_tensor_tensor(
                                    out=sin_[:, d_, cb:cb + 1], in0=sin_[:, d_, cb:cb + 1], scalar=aeff[:, 0, r:r + 1], in1=heff[:, 0, r:r + 1],
                                    op0=ALU.mult, op1=ALU.add), reads=["sin", "aeff", "heff"], writes=["sin"])
                    for cb in range(2):
                        for d_ in range(2):
                            coeffs(d_, cb)
                            rev = (d_ == 1)
                            dst = hsum if d_ == 0 else hscr
                            dk = "hsum" if d_ == 0 else "hscr"
                            scan(dst[:, TL:NT], au[:, 0, TL:NT], au[:, 1, TL:NT], 0.0, rev, ["a", "u"], [dk])
                            scan(dst[:, 0:TL], au[:, 0, 0:TL], au[:, 1, 0:TL], sin_[:, d_, cb:cb + 1], rev, ["a", "u", "sin"], [dk])
                        S.op("dve", lambda: nc.vector.tensor_tensor(out=hsum[:], in0=hsum[:], in1=hscr[:], op=ALU.add),
                             reads=["hsum", "hscr"], writes=["hsum"])
                        S.op("dve", lambda cb=cb: nc.vector.tensor_tensor(out=rT[:, cb, :], in0=hsum[:], in1=lg[:, cb, :], op=ALU.mult),
                             reads=["hsum", "lg"], writes=["rT"])
                    tg = [(0, i * 512, 512, i * 512) for i in range(4)]
                    if not last:
                        tg.append((1, 0, CT, TL))
                    ny = 0
                    for (stream, t0, n, o0) in tg:
                        for j in range(8):
                            py = 3 + (ny % 2)
                            ny += 1
                            for cb in range(2):
                                S.op("pe", lambda cb=cb, j=j, py=py, n=n, o0=o0: nc.tensor.matmul(
                                    psum[:, py, 0:n], lhsT=wob[:, cb, j * 128:(j + 1) * 128], rhs=rT[:, cb, o0:o0 + n],
                                    start=(cb == 0), stop=(cb == 1)), reads=["wob", "rT"], writes=[PS(py)])
                            S.op("dve", lambda py=py, stream=stream, t0=t0, n=n, j=j: nc.vector.scalar_tensor_tensor(
                                out=hap(stream, j, t0, n), in0=psum[:, py, 0:n], scalar=consts[:, 1, stream, 2, j:j + 1],
                                in1=hap(stream, j, t0, n), op0=ALU.mult, op1=ALU.add),
                                reads=[PS(py), hkey(stream, j), "consts"], writes=[hkey(stream, j)])
                    S.barrier()

            with ExitStack() as st:
              if 'p4' in ST:
                NR = 4
                kring = sb("a_kring", [128, NR, TL], BF16, st)
                vring = sb("a_vring", [128, NR, 16, 128], BF16, st)
                qh = sb("a_qh", [128, 2, NT], BF16, st)
                NPT = 4
                pt = sb("a_pt", [128, NPT, 1024], BF16, st)
                acc = sb("a_acc", [128, 2, 1024], F32, st)
                accb = sb("a_accb", [128, 1024], BF16, st)
                SBK = [(0, 1), (2, 3), (6, 7)]
                f1 = sb("a_f1", [128, 512], F32, st)
                f2 = sb("a_f2", [128, 512], F32, st)
                f3 = sb("a_f3", [128, 512], F32, st)
                sqb = sb("a_sqb", [128, 512], BF16, st)
                onb = sb("a_onb", [128, 512], BF16, st)
                wo = sb("a_wo", [128, 2, 1024], BF16, st)
                nu = [0]
                npt = [0]
                ns = [0]
                xa = xch_all.ap()
                kc_v = kc_dram.ap()
                vc_v = vc_dram.ap().rearrange("(u i p) d -> u p i d", i=2, p=128)
                sched = []
                for hb in range(6):
                    qgroups = [(0, i * 512, 512, i * 512) for i in range(4)]
                    if not last:
                        qgroups.append((1, 0, CT, TL))
                    for qg in qgroups:
                        units = [("c", 0)] + ([("r", r) for r in range(NCORES)] if qg[0] == 0 else [])
                        sched.append((hb, qg, units))
                flat = []
                for (hb, qg, units) in sched:
                    for (ukind, r) in units:
                        flat.append((min(hb, 4), ukind, r))

                def load_unit(g):
                    if g >= len(flat):
                        return
                    kb_, ukind, r = flat[g]
                    slot = g % NR
                    if ukind == "c":
                        S.dma("sp", kring[:, slot, 0:CT], kc_v[kb_ * 128:(kb_ + 1) * 128, :], reads=["kc_dram"], writes=[("kr", slot)])
                        S.dma("sp", vring[:, slot, 0:2, :], vc_v[kb_], reads=["vc_dram"], writes=[("vr", slot)])
                    else:
                        k0_ = r * XR + kb_ * 128
                        S.dma("sp", kring[:, slot, :], xa[k0_:k0_ + 128, :], reads=["xch_all"], writes=[("kr", slot)])
                        vsrc = xa[k0_ + 640:k0_ + 768, :].rearrange("q (x d) -> (q x) d", d=128).rearrange("(i p) d -> p i d", p=128)
                        S.dma("sp", vring[:, slot, :, :], vsrc, reads=["xch_all"], writes=[("vr", slot)])

                load_unit(0)
                load_unit(1)
                gbase = 0
                prev_hb = -1
                for (hb, (stream, t0, n, qo), units) in sched:
                    kb = min(hb, 4)
                    is_da = hb < 4
                    qs = hb % 2
                    if hb != prev_hb:
                        prev_hb = hb
                        S.dma("sp", qh[:, qs, :], q_dram[hb * 128:(hb + 1) * 128, :], reads=["q_dram"], writes=[("qh", qs)])
                        S.dma("pool", wo[:, qs, :], wout_d[l, hb], writes=[("wo", qs)])
                    if True:
                        tiles = []
                        for ui, (ukind, r) in enumerate(units):
                            g = gbase + ui
                            slot = g % NR
                            nt_ = 2 if ukind == "c" else 16
                            tiles += [(slot, i, (g if i == 0 else -1)) for i in range(nt_)]
                        gbase += len(units)
                        NTI = len(tiles)

                        def qk(ti):
                            slot, i, gfirst = tiles[ti]
                            if gfirst >= 1:
                                load_unit(gfirst + 1)
                            sbuf_ = SBK[ns[0] % 3]
                            ns[0] += 1
                            for m in range(2):
                                S.op("pe", lambda m=m, slot=slot, i=i, sbuf_=sbuf_: nc.tensor.matmul(
                                    psum[:, sbuf_[m], 0:n], lhsT=kring[m * 64:(m + 1) * 64, slot, i * 128:(i + 1) * 128],
                                    rhs=qh[m * 64:(m + 1) * 64, qs, qo:qo + n], start=True, stop=True),
                                    reads=[("kr", slot), ("qh", qs)], writes=[PS(sbuf_[m])])
                            return sbuf_

                        pendq = [qk(t_) for t_ in range(min(2, NTI))]
                        for ti in range(NTI):
                            sbuf_ = pendq.pop(0)
                            if ti + 2 < NTI:
                                pendq.append(qk(ti + 2))
                            slot, i, _g = tiles[ti]
                            ps_ = npt[0] % NPT
                            npt[0] += 1
                            ptv = pt[:, ps_, :].rearrange("p (m q) -> p m q", m=2)[:, :, 0:n]
                            S.op("act", lambda sbuf_=sbuf_, ps_=ps_, ptv=ptv: nc.scalar.activation(
                                out=ptv, in_=psum[:, sbuf_[0]:sbuf_[0] + 2, 0:n],
                                func=AF.Exp, scale=0.125), reads=[PS(sbuf_[0]), PS(sbuf_[1])], writes=[("pt", ps_)])
                            first = (ti == 0)
                            lastt = (ti == NTI - 1)
                            for m in range(2):
                                prhs = pt[:, ps_, m * 512:m * 512 + n]
                                if is_da:
                                    S.op("pe", lambda m=m, slot=slot, i=i, prhs=prhs: nc.tensor.matmul(
                                        psum[:, 4 + m, 0:n], lhsT=vring[:, slot, i, :], rhs=prhs, start=first, stop=lastt),
                                        reads=[("vr", slot), ("pt", ps_)], writes=[PS(4 + m)])
                                else:
                                    S.op("pe", lambda m=m, slot=slot, i=i, prhs=prhs: nc.tensor.matmul(
                                        psum[m * 64:(m + 1) * 64, 4, 0:n], lhsT=vring[:, slot, i, m * 64:(m + 1) * 64], rhs=prhs,
                                        start=first, stop=lastt), reads=[("vr", slot), ("pt", ps_)], writes=[PS(4)])
                            aj = ti % 2
                            av = acc[:, aj, :].rearrange("p (m q) -> p m q", m=2)[:, :, 0:n]
                            if ti < 2:
                                S.op("dve", lambda av=av, ptv=ptv: nc.vector.tensor_copy(out=av, in_=ptv),
                                     reads=[("pt", ps_)], writes=[("acc", aj)])
                            else:
                                S.op("dve", lambda av=av, ptv=ptv: nc.vector.tensor_tensor(out=av, in0=av, in1=ptv, op=ALU.add),
                                     reads=[("pt", ps_), ("acc", aj)], writes=[("acc", aj)])
                        S.op("dve", lambda: nc.vector.tensor_tensor(
                            out=accb[:].rearrange("p (m q) -> p m q", m=2)[:, :, 0:n],
                            in0=acc[:, 0, :].rearrange("p (m q) -> p m q", m=2)[:, :, 0:n],
                            in1=acc[:, 1, :].rearrange("p (m q) -> p m q", m=2)[:, :, 0:n], op=ALU.add),
                            reads=[("acc", 0), ("acc", 1)], writes=["accb"])
                        for m in range(2):
                            if is_da:
                                S.op("pe", lambda m=m: nc.tensor.matmul(psum[:, 6 + m, 0:n], lhsT=ones[:], rhs=accb[:, m * 512:m * 512 + n],
                                                                        start=True, stop=True), reads=["ones", "accb"], writes=[PS(6 + m)])
                            else:
                                S.op("pe", lambda m=m: nc.tensor.matmul(psum[m * 64:(m + 1) * 64, 6, 0:n], lhsT=ones[:, 0:64],
                                                                        rhs=accb[:, m * 512:m * 512 + n], start=True, stop=True),
                                     reads=["ones", "accb"], writes=[PS(6)])
                        if is_da:
                            S.op("dve", lambda: nc.vector.reciprocal(out=f1[:, 0:n], in_=psum[:, 6, 0:n]), reads=[PS(6)], writes=["f1"])
                            S.op("dve", lambda: nc.vector.tensor_tensor(out=f1[:, 0:n], in0=psum[:, 4, 0:n], in1=f1[:, 0:n], op=ALU.mult),
                                 reads=[PS(4), "f1"], writes=["f1"])
                            S.op("dve", lambda: nc.vector.reciprocal(out=f2[:, 0:n], in_=psum[:, 7, 0:n]), reads=[PS(7)], writes=["f2"])
                            S.op("dve", lambda: nc.vector.tensor_tensor(out=f2[:, 0:n], in0=psum[:, 5, 0:n], in1=f2[:, 0:n], op=ALU.mult),
                                 reads=[PS(5), "f2"], writes=["f2"])
                            S.op("dve", lambda: nc.vector.scalar_tensor_tensor(out=f1[:, 0:n], in0=f2[:, 0:n], scalar=lamneg[:, l:l + 1],
                                                                               in1=f1[:, 0:n], op0=ALU.mult, op1=ALU.add),
                                 reads=["f1", "f2", "lamneg"], writes=["f1"])
                            S.op("act", lambda: nc.scalar.activation(out=sqb[:, 0:n], in_=f1[:, 0:n], func=AF.Square), reads=["f1"], writes=["sqb"])
                            S.op("pe", lambda: nc.tensor.matmul(psum[:, 0, 0:n], lhsT=ones[:], rhs=sqb[:, 0:n], start=True, stop=True),
                                 reads=["ones", "sqb"], writes=[PS(0)])
                            S.op("act", lambda: nc.scalar.activation(out=f3[:, 0:n], in_=psum[:, 0, 0:n], func=AF.Ln, bias=EPS, scale=1.0 / 128),
                                 reads=[PS(0)], writes=["f3"])
                            S.op("act", lambda: nc.scalar.activation(out=f3[:, 0:n], in_=f3[:, 0:n], func=AF.Exp, scale=-0.5), reads=["f3"], writes=["f3"])
                            S.op("dve", lambda: nc.vector.scalar_tensor_tensor(out=onb[:, 0:n], in0=f1[:, 0:n], scalar=gsub[:, l:l + 1],
                                                                               in1=f3[:, 0:n], op0=ALU.mult, op1=ALU.mult),
                                 reads=["f1", "f3", "gsub"], writes=["onb"])
                        else:
                            S.op("dve", lambda: nc.vector.reciprocal(out=f1[:, 0:n], in_=psum[:, 6, 0:n]), reads=[PS(6)], writes=["f1"])
                            S.op("dve", lambda: nc.vector.tensor_tensor(out=onb[:, 0:n], in0=psum[:, 4, 0:n], in1=f1[:, 0:n], op=ALU.mult),
                                 reads=[PS(4), "f1"], writes=["onb"])
                        for j in range(8):
                            py = (j % 2)
                            S.op("pe", lambda j=j, py=py: nc.tensor.matmul(psum[:, py, 0:n], lhsT=wo[:, qs, j * 128:(j + 1) * 128], rhs=onb[:, 0:n],
                                                                           start=True, stop=True), reads=[("wo", qs), "onb"], writes=[PS(py)])
                            S.op("dve", lambda j=j, py=py: nc.vector.scalar_tensor_tensor(
                                out=hap(stream, j, t0, n), in0=psum[:, py, 0:n], scalar=consts[:, 1, stream, 2, j:j + 1],
                                in1=hap(stream, j, t0, n), op0=ALU.mult, op1=ALU.add),
                                reads=[PS(py), hkey(stream, j), "consts"], writes=[hkey(stream, j)])
                S.barrier()

        for l in range(depth):
            last = (l == DEPTH - 1)
            if 'mods' in ST:
                emit_mods(l)
            if 'ffn1' in ST:
                emit_ffn(l, 0, True)
            emit_mixer(l, last)
            if 'ffn2' in ST:
                emit_ffn(l, 1, not last)

        with ExitStack() as st:
            tmp = make_tmp(st)
            ob = sb("o_ob", [128, 2, 512], F32, st)
            no = 0
            for sg in range(4):
                t0 = sg * 512
                rstd = tmp["rstd"][:, 0:512]
                emit_rstd(0, t0, 512, tmp, rstd, 0)
                for c in range(8):
                    o_ = no % 2
                    no += 1
                    S.op("dve", lambda c=c, o_=o_, t0=t0: nc.vector.scalar_tensor_tensor(
                        out=ob[:, o_, :], in0=hT[:, c, t0:t0 + 512], scalar=fing[:, c:c + 1], in1=rstd, op0=ALU.mult, op1=ALU.mult),
                        reads=[hkey(0, c), "rstd", "fing"], writes=[("ob", o_)])
                    S.dma("sp", out_d[:, c, t0:t0 + 512], ob[:, o_, :], reads=[("ob", o_)], writes=["out"])
            S.barrier()
    return nc


def _partner():
    p = np.zeros(64, np.int64)
    for d in range(64):
        j = d % 32
        p[d] = d + 16 if j < 16 else d - 16
    return p


def _rope_tables():
    half = 16
    inv = (10000.0 ** (-np.arange(half, dtype=np.float32) / half)).astype(np.float32)
    t = np.arange(SEQ)
    row = (t // GRID_W).astype(np.float32)
    col = (t % GRID_W).astype(np.float32)
    ang = np.stack([row[:, None] * inv, col[:, None] * inv], 0).astype(np.float32)
    cos = np.cos(ang).astype(np.float32)
    sin = np.sin(ang).astype(np.float32)
    cosT = np.zeros((64, SEQ), np.float32)
    sinT = np.zeros((64, SEQ), np.float32)
    for d in range(64):
        a = d // 32
        j = d % 32
        cosT[d] = cos[a, :, j % 16]
        sinT[d] = -sin[a, :, j] if j < 16 else sin[a, :, j - 16]
    return np.concatenate([cosT, cosT], 0), np.concatenate([sinT, sinT], 0)


def _fm(v, nch=8):
    s = v.shape[:-1]
    return np.ascontiguousarray(np.moveaxis(v.reshape(s + (nch, 128)), -1, 0))


def prep_inputs(inp, depth=DEPTH):
    L = DEPTH
    f = lambda a: np.ascontiguousarray(np.asarray(a, dtype=np.float32))
    x = f(inp["x"])[0]
    ctx = f(inp["ctx"])[0]
    shared = {}
    shared["cxT"] = np.ascontiguousarray(ctx.T.reshape(8, 128, CT).transpose(1, 0, 2))
    cc = np.stack([f(inp["c"])[0], f(inp["c_ctx"])], -1)
    shared["scT"] = np.ascontiguousarray(cc.reshape(8, 128, 2).transpose(1, 0, 2))
    aw = f(inp["ada_w"]).reshape(L, 8, 128, 72, 128)
    shared["ada_w"] = np.ascontiguousarray(aw.transpose(0, 3, 2, 1, 4))
    shared["ada_b"] = np.ascontiguousarray(f(inp["ada_b"]).reshape(L, 72, 128).transpose(2, 0, 1))
    shared["norm_g"] = np.ascontiguousarray(f(inp["norm_g"]).reshape(L, 3, 8, 128).transpose(3, 0, 1, 2))
    shared["final_g"] = np.ascontiguousarray(f(inp["final_g"]).reshape(8, 128).T)
    for i, nm in enumerate(("ffn1", "ffn2")):
        w13 = f(inp[nm + "_w13"]).reshape(L, 8, 128, 2, NFC, 128)
        shared["w13_%d" % (i + 1)] = np.ascontiguousarray(w13.transpose(0, 4, 2, 1, 3, 5).reshape(L, NFC, 128, 8, 256))
        w2 = f(inp[nm + "_w2"]).reshape(L, NFC, 128, 8, 128)
        shared["w2_%d" % (i + 1)] = np.ascontiguousarray(w2.transpose(0, 3, 2, 1, 4))
    win = f(inp["w_in"])
    part = _partner()
    blocks = []

    def rot(cols):
        idx = np.concatenate([part, 64 + part])
        return cols[:, :, idx]

    aq = [win[:, :, h * 128:(h + 1) * 128] for h in range(4)]
    ak = [win[:, :, 512 + h * 128:512 + (h + 1) * 128] for h in range(4)]
    bq_heads = [win[:, :, 1536 + h * 64:1536 + (h + 1) * 64] for h in range(4)]
    bq = [np.concatenate([bq_heads[p], bq_heads[2 + p]], -1) for p in range(2)]
    bk = [win[:, :, 1792:1920]]
    lxw = [win[:, :, 2048 + cb * 128:2048 + (cb + 1) * 128] for cb in range(2)]
    lgw = [win[:, :, 2304 + cb * 128:2304 + (cb + 1) * 128] for cb in range(2)]
    blocks = aq + [rot(b) for b in aq] + ak + [rot(b) for b in ak] + bq + [rot(b) for b in bq] + bk + [rot(b) for b in bk] + lxw + lgw
    wb = np.stack(blocks, 1)
    shared["w_in"] = np.ascontiguousarray(wb.reshape(L, NBLK_IN, 8, 128, 128).transpose(0, 1, 3, 2, 4))
    wvv = np.concatenate([win[:, :, 1024:1536], win[:, :, 1920:2048]], -1)
    shared["w_v"] = np.ascontiguousarray(wvv.reshape(L, 8, 128, 640).transpose(0, 2, 1, 3))
    wo = f(inp["w_out"])
    rows = [np.arange(h * 128, (h + 1) * 128) for h in range(4)]
    for p in range(2):
        rows.append(np.concatenate([512 + p * 64 + np.arange(64), 512 + (2 + p) * 64 + np.arange(64)]))
    rows += [np.arange(768, 896), np.arange(896, 1024)]
    shared["w_out"] = np.ascontiguousarray(np.stack([wo[:, r, :] for r in rows], 1))
    shared["da_lam"] = np.ascontiguousarray(f(inp["da_lam"]).reshape(1, L * 256))
    shared["subln_g"] = np.ascontiguousarray(f(inp["da_subln_g"]).T)
    qg = f(inp["qk_norm_g"])
    idx = np.concatenate([np.arange(64), np.arange(64)])
    idxr = np.concatenate([part, part])
    qkg = np.stack([qg[:, 0, idx], qg[:, 0, idxr], qg[:, 1, idx], qg[:, 1, idxr]], -1)
    shared["qkg"] = np.ascontiguousarray(qkg.transpose(1, 0, 2))
    cw = f(inp["lru_conv_w"]).reshape(L, 4, 2, 128)
    shared["conv_w"] = np.ascontiguousarray(cw.transpose(3, 0, 2, 1))
    shared["conv_b"] = np.ascontiguousarray(f(inp["lru_conv_b"]).reshape(L, 2, 128).transpose(2, 0, 1))
    for nm, key in (("lru_wa", "wa_bd"), ("lru_wi", "wi_bd")):
        w = f(inp[nm])
        bd = np.zeros((L, 2, 2, 128, 128), np.float32)
        for cb in range(2):
            for k in range(2):
                bd[:, :, cb, k * 64:(k + 1) * 64, k * 64:(k + 1) * 64] = w[:, :, cb * 2 + k]
        shared[key] = bd
    lb = np.stack([f(inp["lru_ba"]), f(inp["lru_bi"]), f(inp["lru_lambda"])], 1)
    shared["lru_b"] = np.ascontiguousarray(lb.reshape(L, 3, 2, 2, 128).transpose(4, 0, 1, 2, 3))
    cosT, sinT = _rope_tables()
    for k_ in ("ada_w", "w13_1", "w13_2", "w2_1", "w2_2", "w_in", "w_v", "w_out", "wa_bd", "wi_bd"):
        shared[k_] = np.ascontiguousarray(shared[k_][:depth])
    in_maps = []
    for c in range(NCORES):
        m = dict(shared)
        xs = x[c * TL:(c + 1) * TL]
        m["xT"] = np.ascontiguousarray(xs.T.reshape(8, 128, TL).transpose(1, 0, 2))
        m["cosT"] = np.ascontiguousarray(cosT[:, c * TL:(c + 1) * TL])
        m["sinT"] = np.ascontiguousarray(sinT[:, c * TL:(c + 1) * TL])
        mk = np.zeros((6, 8), np.float32)
        r = np.arange(8)
        mk[0] = (r == c - 1)
        mk[1] = (r == c + 1)
        mk[2] = (r < c)
        mk[3] = 1.0 - mk[2]
        mk[4] = (r > c)
        mk[5] = 1.0 - mk[4]
        m["masks"] = np.ascontiguousarray(np.broadcast_to(mk[None], (128, 6, 8)))
        in_maps.append(m)
    return in_maps


_NC_CACHE = {}


def kernel(**inputs):
    in_maps = prep_inputs(inputs)
    if "nc" not in _NC_CACHE:
        _NC_CACHE["nc"] = build_program()
    nc = _NC_CACHE["nc"]
    res = run_bass_kernel_spmd(nc, in_maps, core_ids=list(range(NCORES)))
    out = np.empty((1, SEQ, D), np.float32)
    for c in range(NCORES):
        oT = np.asarray(res.results[c]["outT"])
        out[0, c * TL:(c + 1) * TL, :] = oT.transpose(2, 1, 0).reshape(TL, D)
    return out
```
